# Optimizing a Trainium2 kernel written in Bass

```python
import math
import jax, jax.numpy as jnp
from jax import lax
import numpy as np

D_MODEL = 4096
BATCH = 4
SEQ = 2048
DEPTH = 1
DEC_BATCH = 128
DEC_SEQ = 1
PAST_LEN = 8192
PAGE_SIZE = 128

MIX_WIDTH = D_MODEL
ATTN_WIDTH = MIX_WIDTH // 2
POOL_WIDTH = MIX_WIDTH - ATTN_WIDTH
HEAD_DIM = 64
N_HEADS = ATTN_WIDTH // HEAD_DIM
N_KV_HEADS = N_HEADS // 8
GROUP = N_HEADS // N_KV_HEADS
KV_WIDTH = N_KV_HEADS * HEAD_DIM
WINDOW = 128
BLOCK = 128
POOL_WINDOWS = (2, 4, 8, 16)
N_POOL_GROUPS = len(POOL_WINDOWS)
POOL_GROUP_WIDTH = POOL_WIDTH // N_POOL_GROUPS
POOL_STATE_ROWS = max(POOL_WINDOWS) - 1
D_FF = ((8 * D_MODEL + 3 * 256 - 1) // (3 * 256)) * 256
QKVU_WIDTH = ATTN_WIDTH + 2 * KV_WIDTH + POOL_WIDTH
NORM_EPS = 1e-5
NEG_INF = -1e30

kernel_name = "hymba_pool_swa_sink_decoder_step"


def rmsnorm(x, g):
    xf = x.astype(jnp.float32)
    xf = xf * lax.rsqrt(jnp.mean(xf * xf, axis=-1, keepdims=True) + NORM_EPS)
    return (xf * g.astype(jnp.float32)).astype(x.dtype)


def split_proj(h, w_in):
    B, L, _ = h.shape
    p = jnp.einsum("bld,de->ble", h, w_in)
    q = p[..., :ATTN_WIDTH].reshape(B, L, N_KV_HEADS, GROUP, HEAD_DIM)
    k = p[..., ATTN_WIDTH:ATTN_WIDTH + KV_WIDTH].reshape(B, L, N_KV_HEADS, HEAD_DIM)
    v = p[..., ATTN_WIDTH + KV_WIDTH:ATTN_WIDTH + 2 * KV_WIDTH].reshape(B, L, N_KV_HEADS, HEAD_DIM)
    u = p[..., ATTN_WIDTH + 2 * KV_WIDTH:]
    return q, k, v, u


def sink_softmax(s, mask, sink):
    s = jnp.where(mask, s.astype(jnp.float32), NEG_INF)
    sink = sink.astype(jnp.float32)
    m = jnp.maximum(jnp.max(s, axis=-1, keepdims=True), sink)
    e = jnp.exp(s - m)
    return e / (jnp.sum(e, axis=-1, keepdims=True) + jnp.exp(sink - m))


def swa_prompt(q, k, v, sinks):
    B, S = q.shape[:2]
    nb = S // BLOCK
    qb = q.reshape(B, nb, BLOCK, N_KV_HEADS, GROUP, HEAD_DIM)
    kb = k.reshape(B, nb, BLOCK, N_KV_HEADS, HEAD_DIM)
    vb = v.reshape(B, nb, BLOCK, N_KV_HEADS, HEAD_DIM)
    pad = ((0, 0), (1, 0), (0, 0), (0, 0), (0, 0))
    kk = jnp.concatenate([jnp.pad(kb, pad)[:, :nb], kb], axis=2)
    vv = jnp.concatenate([jnp.pad(vb, pad)[:, :nb], vb], axis=2)
    s = jnp.einsum("bnqkgd,bnskd->bnkgqs", qb, kk) * (HEAD_DIM ** -0.5)
    qi = jnp.arange(BLOCK)[:, None]
    kj = jnp.arange(2 * BLOCK)[None, :] - BLOCK
    band = (kj <= qi) & (qi - kj < WINDOW)
    valid = (jnp.arange(nb)[:, None, None] * BLOCK + kj[None]) >= 0
    mask = (band[None] & valid)[None, :, None, None]
    p = sink_softmax(s, mask, sinks.reshape(1, 1, N_KV_HEADS, GROUP, 1, 1))
    o = jnp.einsum("bnkgqs,bnskd->bnqkgd", p.astype(vv.dtype), vv)
    return o.reshape(B, S, ATTN_WIDTH)


def swa_decode(q, k_new, v_new, cache_k, cache_v, sinks):
    B, T = q.shape[:2]
    W = cache_k.shape[1]
    kk = jnp.concatenate([cache_k, k_new], axis=1)
    vv = jnp.concatenate([cache_v, v_new], axis=1)
    pos_q = PAST_LEN + jnp.arange(T)
    pos_k = PAST_LEN - W + jnp.arange(W + T)
    mask = (pos_k[None, :] <= pos_q[:, None]) & (pos_q[:, None] - pos_k[None, :] < WINDOW)
    s = jnp.einsum("bqkgd,bskd->bkgqs", q, kk) * (HEAD_DIM ** -0.5)
    p = sink_softmax(s, mask[None, None, None], sinks.reshape(1, N_KV_HEADS, GROUP, 1, 1))
    o = jnp.einsum("bkgqs,bskd->bqkgd", p.astype(vv.dtype), vv)
    return o.reshape(B, T, ATTN_WIDTH), kk[:, -W:], vv[:, -W:]


def pool_mix(u, pos, w_pool, pool_scale):
    B, L, _ = u.shape
    ug = u.reshape(B, L, N_POOL_GROUPS, POOL_GROUP_WIDTH).astype(jnp.float32)
    outs = []
    for g, w in enumerate(POOL_WINDOWS):
        x = ug[:, :, g]
        cs = jnp.cumsum(x, axis=1)
        shifted = jnp.pad(cs, ((0, 0), (w, 0), (0, 0)))[:, :L]
        cnt = jnp.minimum(pos + 1, w).astype(jnp.float32)[None, :, None]
        outs.append((cs - shifted) / cnt - x)
    d = jnp.stack(outs, axis=2).astype(u.dtype)
    y = jnp.einsum("blgc,gce->blge", d, w_pool).reshape(B, L, POOL_WIDTH)
    return y * pool_scale


def post_mix(x, a, pooled, w_o, g_ffn, w_gate, w_up, w_down):
    x = x + jnp.einsum("ble,ed->bld", jnp.concatenate([a, pooled], axis=-1), w_o)
    h = rmsnorm(x, g_ffn)
    ff = jax.nn.silu(jnp.einsum("bld,df->blf", h, w_gate)) * jnp.einsum("bld,df->blf", h, w_up)
    return x + jnp.einsum("blf,fd->bld", ff, w_down)


def setup_inputs(seed: int = 0) -> dict:
    key = jax.random.key(seed)
    ks = jax.random.split(key, 20)
    f32 = jnp.float32
    n = lambda k, shape, scale: (jax.random.normal(k, shape, f32) * scale)
    w_rows = min(WINDOW, PAST_LEN)
    return {
        "x_prompt": n(ks[0], (BATCH, SEQ, D_MODEL), 1.0),
        "x_sample": n(ks[1], (DEC_BATCH, DEC_SEQ, D_MODEL), 1.0),
        "cache_k_win": n(ks[2], (DEPTH, DEC_BATCH, w_rows, N_KV_HEADS, HEAD_DIM), 1.0),
        "cache_v_win": n(ks[3], (DEPTH, DEC_BATCH, w_rows, N_KV_HEADS, HEAD_DIM), 1.0),
        "state_pool": n(ks[4], (DEPTH, DEC_BATCH, POOL_STATE_ROWS, POOL_WIDTH), 1.0),
        "g_mix": 1.0 + n(ks[5], (DEPTH, D_MODEL), 0.02),
        "w_in": n(ks[6], (DEPTH, D_MODEL, QKVU_WIDTH), D_MODEL ** -0.5),
        "sinks": n(ks[7], (DEPTH, N_HEADS), 0.5),
        "w_pool": n(ks[8], (DEPTH, N_POOL_GROUPS, POOL_GROUP_WIDTH, POOL_GROUP_WIDTH), POOL_GROUP_WIDTH ** -0.5),
        "pool_scale": 1.0 + n(ks[9], (DEPTH, POOL_WIDTH), 0.02),
        "w_o": n(ks[10], (DEPTH, MIX_WIDTH, D_MODEL), MIX_WIDTH ** -0.5),
        "g_ffn": 1.0 + n(ks[11], (DEPTH, D_MODEL), 0.02),
        "w_gate": n(ks[12], (DEPTH, D_MODEL, D_FF), D_MODEL ** -0.5),
        "w_up": n(ks[13], (DEPTH, D_MODEL, D_FF), D_MODEL ** -0.5),
        "w_down": n(ks[14], (DEPTH, D_FF, D_MODEL), D_FF ** -0.5),
        "g_final": 1.0 + n(ks[15], (D_MODEL,), 0.02),
    }


def reference(x_prompt, x_sample, cache_k_win, cache_v_win, state_pool, g_mix, w_in, sinks,
              w_pool, pool_scale, w_o, g_ffn, w_gate, w_up, w_down, g_final):
    S = x_prompt.shape[1]
    T = x_sample.shape[1]
    pw = min(WINDOW, S)
    pos_prompt = jnp.arange(S)
    pos_sample_ext = PAST_LEN - POOL_STATE_ROWS + jnp.arange(POOL_STATE_ROWS + T)
    xp, xs = x_prompt, x_sample
    kp_l, vp_l, pp_l, ks_l, vs_l, ps_l = [], [], [], [], [], []
    for l in range(DEPTH):
        q, k, v, u = split_proj(rmsnorm(xp, g_mix[l]), w_in[l])
        a = swa_prompt(q, k, v, sinks[l])
        pooled = pool_mix(u, pos_prompt, w_pool[l], pool_scale[l])
        xp = post_mix(xp, a, pooled, w_o[l], g_ffn[l], w_gate[l], w_up[l], w_down[l])
        kp_l.append(k[:, -pw:])
        vp_l.append(v[:, -pw:])
        pp_l.append(u[:, -POOL_STATE_ROWS:])
        q, k, v, u = split_proj(rmsnorm(xs, g_mix[l]), w_in[l])
        a, k_win, v_win = swa_decode(q, k, v, cache_k_win[l], cache_v_win[l], sinks[l])
        u_ext = jnp.concatenate([state_pool[l], u], axis=1)
        pooled = pool_mix(u_ext, pos_sample_ext, w_pool[l], pool_scale[l])[:, POOL_STATE_ROWS:]
        xs = post_mix(xs, a, pooled, w_o[l], g_ffn[l], w_gate[l], w_up[l], w_down[l])
        ks_l.append(k_win)
        vs_l.append(v_win)
        ps_l.append(u_ext[:, -POOL_STATE_ROWS:])
    y_prompt = rmsnorm(xp, g_final)
    y_sample = rmsnorm(xs, g_final)
    return (y_prompt, y_sample, jnp.stack(kp_l), jnp.stack(vp_l), jnp.stack(pp_l),
            jnp.stack(ks_l), jnp.stack(vs_l), jnp.stack(ps_l))
```

```python
import numpy as np
from contextlib import ExitStack
import concourse.bass as bass
import concourse.mybir as mybir
from concourse.bass_utils import run_bass_kernel_spmd

F32 = mybir.dt.float32
BF16 = mybir.dt.bfloat16
AF = mybir.ActivationFunctionType
ALU = mybir.AluOpType
AX = mybir.AxisListType

PE, ACT, DVE, POOL, SP = "tensor", "scalar", "vector", "gpsimd", "sync"
COMPUTE = (PE, ACT, DVE, POOL)

NCORES = 8
D = 4096
NCH = 32
DFF = 11008
NFC = 86
NPASS = 2
TP = 512
NS = 8
HALO = 128
TOK = HALO + TP + NS
TQ = TP + NS
NQ = TQ // 2
UT = 16 + TQ
EPS = 1e-5
POOLW = (2, 4, 8, 16)
G_ACC = 4
NUNIT = 10
MASKNEG = -30000.0


class Buf:
    __slots__ = ("name", "writer", "readers", "rg")

    def __init__(self, name="", rg=None):
        self.name = name
        self.writer = None
        self.readers = []
        self.rg = rg


class Op:
    __slots__ = ("eng", "fn", "pos", "needs_inc", "waits", "dsem", "dval", "cnt", "label")

    def __init__(self, eng, fn):
        self.label = None
        self.eng = eng
        self.fn = fn
        self.pos = -1
        self.needs_inc = False
        self.waits = []
        self.dsem = None
        self.dval = 0
        self.cnt = 0


class DSem:
    def __init__(self, name):
        self.name = name
        self.val = 0
        self.handle = None


class Prog:
    def __init__(self, same_engine_sync=True):
        self.streams = {e: [] for e in (PE, ACT, DVE, POOL, SP)}
        self.waited = {e: {} for e in (PE, ACT, DVE, POOL, SP)}
        self.dsems = []
        self.same_engine_sync = same_engine_sync
        self.label = None
        self.use_scopes = False
        self.pass_idx = 0

    def dsem(self, name):
        d = DSem(name)
        self.dsems.append(d)
        return d

    def _dep(self, op, src):
        if src is None or src is op:
            return
        e = op.eng
        if src.dsem is not None:
            key = ("d", id(src.dsem))
            pos = src.dval
        else:
            if src.eng == e and op.dsem is None:
                if e == PE or not self.same_engine_sync:
                    return
            key = ("e", src.eng)
            pos = src.pos
        if self.waited[e].get(key, -1) >= pos:
            return
        self.waited[e][key] = pos
        src.needs_inc = True
        op.waits.append(src)

    def op(self, eng, fn, reads=(), writes=(), dsem=None):
        o = Op(eng, fn)
        o.label = self.label
        if dsem is not None:
            o.dsem = dsem
            dsem.val += 16
            o.dval = dsem.val
        o.pos = len(self.streams[eng])
        reads = list(reads)
        writes = list(writes)
        for b in list(reads) + list(writes):
            if b.rg is not None and b.rg not in reads:
                reads.append(b.rg)
        for r in reads:
            self._dep(o, r.writer)
        for w in writes:
            self._dep(o, w.writer)
            for rd in w.readers:
                self._dep(o, rd)
        for r in reads:
            r.readers.append(o)
        for w in writes:
            w.writer = o
            w.readers = []
        self.streams[eng].append(o)
        return o

    def emit(self, nc, es, final_wait_engine=SP):
        esem = {}
        for e in COMPUTE:
            esem[e] = es.enter_context(nc.semaphore("s_" + e))
        for d in self.dsems:
            d.handle = es.enter_context(nc.semaphore("d_" + d.name))
        for e in COMPUTE:
            c = 0
            for o in self.streams[e]:
                if o.dsem is None and o.needs_inc:
                    c += 1
                o.cnt = c
        block = es.enter_context(nc.Block())
        dsems = self.dsems

        def run(e):
            def body(eng):
                cur, cm = None, None
                for o in self.streams[e]:
                    if self.use_scopes and o.label != cur:
                        if cm is not None:
                            cm.__exit__(None, None, None)
                        cm = nc.named_scope(o.label or "none")
                        cm.__enter__()
                        cur = o.label
                    for s in o.waits:
                        if s.dsem is not None:
                            eng.wait_ge(s.dsem.handle, s.dval)
                        else:
                            eng.wait_ge(esem[s.eng], s.cnt)
                    ins = o.fn(eng)
                    if o.dsem is not None:
                        ins.then_inc(o.dsem.handle, 16)
                    elif o.needs_inc:
                        ins.then_inc(esem[e], 1)
                if cm is not None:
                    cm.__exit__(None, None, None)
                if e == final_wait_engine:
                    for d in dsems:
                        if d.val > 0:
                            eng.wait_ge(d.handle, d.val)
            return body

        block.tensor(run(PE))
        block.scalar(run(ACT))
        block.vector(run(DVE))
        block.gpsimd(run(POOL))
        block.sync(run(SP))


class _Stop(Exception):
    pass


def build_program(stop_after=None, scopes=False):
    nc = bass.Bass("TRN2", target_bir_lowering=False)
    P = Prog()
    P.use_scopes = scopes

    def chk(name):
        P.label = "p%d_after_%s" % (P.pass_idx, name)
        if stop_after == name:
            raise _Stop()

    def din(name, shape):
        return nc.dram_tensor(name, list(shape), F32, kind="ExternalInput").ap()

    def dout(name, shape):
        return nc.dram_tensor(name, list(shape), F32, kind="ExternalOutput").ap()

    xh = din("xh", [HALO + NPASS * TP, D])
    xs = din("xs", [NPASS * NS, D])
    ck = din("ck", [NPASS * NS, 128, 256])
    cv = din("cv", [NPASS * NS, 128, 256])
    spool = din("sp", [NPASS * NS, 15, 2048])
    g_mix = din("g_mix", [D])
    g_ffn = din("g_ffn", [D])
    g_fin = din("g_fin", [D])
    w_in = din("w_in", [D, 4608])
    w_pool = din("w_pool", [4, 512, 512])
    pscale = din("pscale", [2048])
    w_o = din("w_o", [D, D])
    w_gate = din("w_gate", [D, DFF])
    w_up = din("w_up", [D, DFF])
    w_down = din("w_down", [DFF, D])
    sinks_d = din("sinks", [32])
    sinkrep_d = din("sinkrep", [128])
    ident_d = din("ident", [128, 128])
    mask0_d = din("mask0", [128, 256])
    maskA_d = din("maskA", [128, 256])
    invc_d = din("invc", [NPASS * 4 * 16])
    sel_d = din("sel", [120, 32])
    onehot_d = din("onehot", [8, 8 * 128])
    selcol_d = din("selcol", [128, 64])

    y_o = dout("y", [NPASS * TP, D])
    ys_o = dout("ys", [NPASS * NS, D])
    kwp_o = dout("kwp", [128, 256])
    vwp_o = dout("vwp", [128, 256])
    poolp_o = dout("poolp", [16, 2048])
    kws_o = dout("kws", [NPASS * NS, 128, 256])
    vws_o = dout("vws", [NPASS * NS, 128, 256])
    pools_o = dout("pools", [NPASS * NS, 15, 2048])

    es = ExitStack()
    with es:
        def sb(name, words):
            return es.enter_context(nc.sbuf_tensor(name, [128, words], F32))

        R1 = sb("R1", 16384)
        R2 = sb("R2", 10368)
        R3 = sb("R3", 6400)
        RXN = sb("RXN", 4096)
        RW = sb("RW", NUNIT * 1024)
        SM = sb("SM", 4352)
        pa = es.enter_context(nc.psum_tensor("pa", [128, 2048], F32))
        pb = es.enter_context(nc.psum_tensor("pb", [128, 2048], F32))

        rgR1, rgR2, rgR3 = Buf("rgR1"), Buf("rgR2"), Buf("rgR3")

        def bank(k):
            t = pa if k < 4 else pb
            return t[:, (k % 4) * 512:(k % 4 + 1) * 512]

        bankB = [Buf(f"bank{k}") for k in range(8)]

        def bf(ap):
            return ap.bitcast(BF16)

        smo = [0]

        def sm(words):
            a = smo[0]
            smo[0] += words
            assert smo[0] <= 4352
            return SM[:, a:a + words]

        a_tok = bf(sm(1024))
        idb = bf(sm(64))
        idf = sm(128)
        gmixT = sm(32)
        gffnT = sm(32)
        psT = sm(16)
        sinkb = sm(32)
        negsinkb = sm(32)
        sinkrep = sm(1)
        negsinkrep = sm(1)
        mask0b = bf(sm(128))
        maskAb = bf(sm(128))
        selt = sm(32)
        onehot = bf(sm(512))
        selcol = bf(sm(32))
        invc = sm(NPASS * 64)
        st = sm(256)
        kst = sm(512)
        vst = sm(512)
        ssum = sm(128)
        ctmp = sm(64)
        sgt = sm(2 * NQ)
        B = {n: Buf(n) for n in ["a_tok", "idb", "idf", "gmixT", "gffnT", "psT", "sinkb", "negsinkb", "sinkrep",
                                 "negsinkrep", "mask0b", "maskAb", "selt", "onehot", "selcol", "invc", "kst", "vst",
                                 "ssum", "ctmp", "sgt0", "sgt1", "RXN", "E", "ET"]}
        stcol = [0]
        rings = {}

        def stat(n=1):
            nsets = {1: 16, 8: 14, 2: 7, 4: 28}[n]
            if n not in rings:
                items = []
                for _ in range(nsets):
                    a = stcol[0]
                    stcol[0] += n
                    assert stcol[0] <= 256, "stats overflow"
                    items.append((st[:, a:a + n], Buf(f"st{a}")))
                rings[n] = [items, 0]
            items, k = rings[n]
            rings[n][1] = k + 1
            return items[k % nsets]

        fz = sm(1)

        def fence(rg):
            P.op(DVE, lambda e: e.memset(fz, 0.0), writes=[rg])

        dctr = [0]

        def nd(name="m"):
            dctr[0] += 1
            return P.dsem(f"{name}{dctr[0]}")

        def ld(eng, out, in_, b, **kw):
            P.op(eng, lambda e: e.dma_start(out=out, in_=in_, **kw), writes=[b], dsem=nd("c"))

        ld(POOL, idb, ident_d, B["idb"])
        ld(SP, idf, ident_d, B["idf"])
        ld(SP, gmixT, g_mix.rearrange("(c p) -> p c", p=128), B["gmixT"], allow_slow_non_contiguous=True)
        ld(SP, gffnT, g_ffn.rearrange("(c p) -> p c", p=128), B["gffnT"], allow_slow_non_contiguous=True)
        ld(SP, psT, pscale.rearrange("(c p) -> p c", p=128), B["psT"], allow_slow_non_contiguous=True)
        ld(SP, sinkb, sinks_d.partition_broadcast(128), B["sinkb"])
        ld(SP, sinkrep, sinkrep_d.rearrange("(p o) -> p o", o=1), B["sinkrep"], allow_slow_non_contiguous=True)
        ld(POOL, mask0b, mask0_d, B["mask0b"])
        ld(POOL, maskAb, maskA_d, B["maskAb"])
        ld(SP, selt[0:120, :], sel_d, B["selt"])
        ld(POOL, onehot[0:8, :], onehot_d, B["onehot"])
        ld(POOL, selcol, selcol_d, B["selcol"])
        ld(SP, invc, invc_d.partition_broadcast(128), B["invc"])
        P.op(DVE, lambda e: e.tensor_scalar(out=negsinkb, in0=sinkb, scalar1=-1.0, scalar2=None, op0=ALU.mult),
             reads=[B["sinkb"]], writes=[B["negsinkb"]])
        P.op(DVE, lambda e: e.tensor_scalar(out=negsinkrep, in0=sinkrep, scalar1=-1.0, scalar2=None, op0=ALU.mult),
             reads=[B["sinkrep"]], writes=[B["negsinkrep"]])

        unitB = [Buf(f"wu{i}") for i in range(NUNIT)]
        unitD = [P.dsem(f"wu{i}") for i in range(NUNIT)]
        upos = [0]

        def wload(units, src_list):
            if upos[0] + units > NUNIT:
                upos[0] = 0
            u0 = upos[0]
            upos[0] += units
            sv = bf(RW[:, u0 * 1024:(u0 + units) * 1024])
            bufs = unitB[u0:u0 + units]
            for (vf, src) in src_list:
                P.op(POOL, lambda e, vf=vf, src=src, sv=sv: e.dma_start(out=vf(sv), in_=src),
                     writes=bufs, dsem=unitD[u0])
            return bufs, sv

        def w_colchunk(wmat, col0):
            src = wmat[:, col0:col0 + 128].rearrange("(c p) e -> p c e", p=128)
            return wload(2, [(lambda sv: sv.rearrange("p (c e) -> p c e", c=32), src)])

        def w_rowhalf(wmat, row0, half):
            return wload(1, [(lambda sv: sv, wmat[row0:row0 + 128, half * 2048:(half + 1) * 2048])])

        xinD = [P.dsem("xin0"), P.dsem("xin1")]

        xnb = bf(RXN[:, 0:2048])
        xnb2 = [xnb, bf(RXN[:, 2048:4096])]
        xnB2 = [B["RXN"], Buf("RXN2")]

        def norm_tile(src_ap, srcB, rows, gT, gB, dstT, dstB, tokoff, trbank, junk=None, junkB=None):
            ss, ssB = stat()
            ms, msB = stat()
            sq, sqB = stat()
            rstd, rsB = stat()
            if junk is None:
                junk, junkB = xnb, B["RXN"]
            P.op(ACT, lambda e: e.activation(out=junk, in_=src_ap, func=AF.Square, accum_out=ss),
                 reads=[srcB], writes=[junkB, ssB])
            P.op(DVE, lambda e: e.tensor_scalar(out=ms, in0=ss, scalar1=1.0 / D, scalar2=EPS,
                                                op0=ALU.mult, op1=ALU.add), reads=[ssB], writes=[msB])
            P.op(ACT, lambda e: e.activation(out=sq, in_=ms, func=AF.Sqrt), reads=[msB], writes=[sqB])
            P.op(DVE, lambda e: e.reciprocal(out=rstd, in_=sq), reads=[sqB], writes=[rsB])
            if dstT is None:
                return rstd, rsB
            P.op(ACT, lambda e: e.activation(out=xnb, in_=src_ap, func=AF.Copy, scale=rstd),
                 reads=[srcB, rsB], writes=[B["RXN"]])
            for gq in range(4):
                bk = trbank[gq % 2]
                pv = bf(bank(bk)).rearrange("p (c t) -> p c t", c=8)

                def tr(e, gq=gq, pv=pv):
                    ins = None
                    for k in range(8):
                        c = gq * 8 + k
                        ins = e.transpose(out=pv[:, k, :], in_=xnb[:, c * 128:(c + 1) * 128], identity=idb)
                    return ins
                P.op(PE, tr, reads=[B["RXN"], B["idb"]], writes=[bankB[bk]])

                def ev(e, gq=gq, pv=pv):
                    return e.tensor_tensor(out=dstT[:, gq * 8:(gq + 1) * 8, tokoff:tokoff + rows],
                                           in0=pv[:, :, 0:rows],
                                           in1=gT[:, gq * 8:(gq + 1) * 8].unsqueeze(2).to_broadcast([128, 8, rows]),
                                           op=ALU.mult)
                P.op(DVE, ev, reads=[bankB[bk], gB], writes=[dstB])
            return rstd, rsB

        def norm_phase(tiles, gT, gB, dstT, dstB, trbank, junk, junkB, pre=None):
            n = len(tiles)
            rst = {}
            for j in range(n + 1):
                if j < n:
                    if pre is not None:
                        pre(j)
                    src_ap, srcB, rows, tokoff = tiles[j]
                    ss, ssB = stat()
                    ms, msB = stat()
                    sq, sqB = stat()
                    rstd, rsB = stat()
                    rst[j] = (rstd, rsB)
                    P.op(ACT, lambda e, src_ap=src_ap, ss=ss: e.activation(out=junk, in_=src_ap, func=AF.Square,
                                                                         accum_out=ss),
                         reads=[srcB], writes=[junkB, ssB])
                    P.op(DVE, lambda e, ms=ms, ss=ss: e.tensor_scalar(out=ms, in0=ss, scalar1=1.0 / D, scalar2=EPS,
                                                                      op0=ALU.mult, op1=ALU.add),
                         reads=[ssB], writes=[msB])
                    P.op(ACT, lambda e, sq=sq, ms=ms: e.activation(out=sq, in_=ms, func=AF.Sqrt), reads=[msB],
                         writes=[sqB])
                    P.op(DVE, lambda e, rstd=rstd, sq=sq: e.reciprocal(out=rstd, in_=sq), reads=[sqB], writes=[rsB])
                if j >= 1:
                    _, _, rows, tokoff = tiles[j - 1]
                    for gq in range(4):
                        bk = trbank[gq % 2]
                        pv = bf(bank(bk)).rearrange("p (c t) -> p c t", c=8)

                        def tr(e, gq=gq, pv=pv, xb=xnb2[(j - 1) % 2]):
                            ins = None
                            for k in range(8):
                                c = gq * 8 + k
                                ins = e.transpose(out=pv[:, k, :], in_=xb[:, c * 128:(c + 1) * 128], identity=idb)
                            return ins
                        P.op(PE, tr, reads=[xnB2[(j - 1) % 2], B["idb"]], writes=[bankB[bk]])

                        def ev(e, gq=gq, pv=pv, rows=rows, tokoff=tokoff):
                            return e.tensor_tensor(out=dstT[:, gq * 8:(gq + 1) * 8, tokoff:tokoff + rows],
                                                   in0=pv[:, :, 0:rows],
                                                   in1=gT[:, gq * 8:(gq + 1) * 8].unsqueeze(2).to_broadcast([128, 8, rows]),
                                                   op=ALU.mult)
                        P.op(DVE, ev, reads=[bankB[bk], gB], writes=[dstB])
                if j < n:
                    src_ap, srcB, rows, tokoff = tiles[j]
                    rstd, rsB = rst[j]
                    P.op(ACT, lambda e, src_ap=src_ap, rstd=rstd, xb=xnb2[j % 2]: e.activation(
                        out=xb, in_=src_ap, func=AF.Copy, scale=rstd), reads=[srcB, rsB], writes=[xnB2[j % 2]])

        def proj_acc(actT_fn, actB, wmat, rows0, x1t, x1B, x1s, x1sB):
            nk = len(rows0)
            it = 0
            for half in range(2):
                wt = [w_rowhalf(wmat, r0, half) for r0 in rows0]
                rb = [actB]
                for (bufs, _) in wt:
                    rb += bufs
                for tj in range(5):
                    rows = 128 if tj < 4 else NS
                    tokoff = tj * 128
                    for dp in range(2):
                        bks = [4 + (it % 2) * 2, 5 + (it % 2) * 2]
                        it += 1

                        def mm(e, tokoff=tokoff, rows=rows, dp=dp, bks=bks, wt=wt):
                            ins = None
                            for q in range(2):
                                dq = dp * 2 + q
                                for k in range(nk):
                                    ins = e.matmul(bank(bks[q])[0:rows, :], lhsT=actT_fn(k)[:, tokoff:tokoff + rows],
                                                   rhs=wt[k][1][:, dq * 512:(dq + 1) * 512], start=(k == 0),
                                                   stop=(k == nk - 1))
                            return ins
                        P.op(PE, mm, reads=rb, writes=[bankB[bks[0]], bankB[bks[1]]])
                        for q in range(2):
                            dt = half * 4 + dp * 2 + q
                            bk = bks[q]
                            if tj < 4:
                                dst = x1t[:, tj, dt * 512:(dt + 1) * 512]
                                dB = x1B[tj]
                            else:
                                dst = x1s[0:NS, dt * 512:(dt + 1) * 512]
                                dB = x1sB
                            P.op(DVE, lambda e, dst=dst, bk=bk, rows=rows: e.tensor_tensor(
                                out=dst, in0=dst, in1=bank(bk)[0:rows, :], op=ALU.add),
                                reads=[bankB[bk]], writes=[dB])

        for p in range(NPASS if stop_after != 'consts' else 0):
          try:
            xt = [R1[:, 0:4096], R1[:, 4096:8192]]
            xtB = [Buf("xt0", rgR1), Buf("xt1", rgR1)]
            uT = R1[:, 0:16 * UT].rearrange("p (c t) -> p c t", c=16)
            uTB = [Buf(f"uT{g}", rgR1) for g in range(4)]
            o1 = 16 * UT
            T1 = R1[:, o1:o1 + 4 * UT].rearrange("p (c t) -> p c t", c=4)
            T2 = R1[:, o1 + 4 * UT:o1 + 8 * UT].rearrange("p (c t) -> p c t", c=4)
            T1B, T2B = Buf("T1", rgR1), Buf("T2", rgR1)
            o2 = o1 + 8 * UT
            dT = [bf(R1[:, o2 + i * 1040:o2 + (i + 1) * 1040]).rearrange("p (c t) -> p c t", c=4) for i in range(2)]
            dTB = [Buf("dT0", rgR1), Buf("dT1", rgR1)]
            assert o2 + 2080 <= 16384
            Kt = R1[:, 0:2048].rearrange("p (b c) -> p b c", b=NS)
            Vt = R1[:, 2048:4096].rearrange("p (b c) -> p b c", b=NS)
            KtB, VtB = Buf("Kt", rgR1), Buf("Vt", rgR1)
            prod = R1[:, 4096:6144]
            prodB = Buf("prod", rgR1)
            prodv = bf(R1[:, 6144:7168])
            prodvB = Buf("prodv", rgR1)
            qtok = bf(R1[:, 7168:8192])
            qtokB = Buf("qtok", rgR1)
            STt = R1[:, 8192:8448]
            STB = Buf("ST", rgR1)
            Ps = R1[:, 8448:8704]
            PsB = Buf("Ps", rgR1)
            PTt = R1[:, 8704:8960]
            PTB = Buf("PT", rgR1)
            Es = R1[:, 8960:9216]
            EsB = Buf("Es", rgR1)
            x1t = R1[:, :].rearrange("p (j d) -> p j d", j=4)
            x1B = [Buf(f"x1_{j}", rgR1) for j in range(4)]

            hT = bf(R2[:, :]).rearrange("p (c t) -> p c t", c=32)
            hTB = Buf("hT", rgR2)
            catT = bf(R2[:, 0:8320]).rearrange("p (c t) -> p c t", c=32)
            catB = Buf("catT", rgR2)
            Eb = bf(R2[:, 8320:9344]).rearrange("p (r s) -> p r s", r=8)
            ETb = bf(R2[:, 9344:10368])
            EB, ETB = Buf("E", rgR2), Buf("ET", rgR2)
            h2T = catT
            h2B = Buf("h2T", rgR2)
            gfin = R2[:, 0:4096]
            gfinB = Buf("gfin", rgR2)

            qT = bf(R3[:, 0:4160]).rearrange("p (c t) -> p c t", c=16)
            kT = bf(R3[:, 4160:5456]).rearrange("p (c t) -> p c t", c=4)
            vtok = bf(R3[:, 5456:6224]).rearrange("p (j c) -> p j c", j=6)
            qTB, kTB, vtB = Buf("qT", rgR3), Buf("kT", rgR3), Buf("vtok", rgR3)
            ff = [bf(R3[:, i * 1040:(i + 1) * 1040]).rearrange("p (c t) -> p c t", c=4) for i in range(2)]
            ffB = [Buf("ff0", rgR3), Buf("ff1", rgR3)]
            x1s = R3[:, 2080:2080 + 4096]
            x1sB = Buf("x1s", rgR3)

            P.pass_idx = p
            P.label = 'p%d_start' % p
            fence(rgR1)
            fence(rgR2)
            fence(rgR3)
            junkA = bf(R3[:, 0:2048])
            junkAB = Buf("junkA", rgR3)
            tilesA = []
            for j in range(6):
                rows = 128 if j < 5 else NS
                tilesA.append((xt[j % 2], xtB[j % 2], rows, 128 * j))

            def loadA(j):
                rows = 128 if j < 5 else NS
                src = xh[p * TP + 128 * j:p * TP + 128 * j + 128, :] if j < 5 else xs[p * NS:(p + 1) * NS, :]
                P.op(SP, lambda e: e.dma_start(out=xt[j % 2][0:rows, :], in_=src), writes=[xtB[j % 2]],
                     dsem=xinD[j % 2])
            norm_phase(tilesA, gmixT, B["gmixT"], hT, hTB, (6, 7), junkA, junkAB, pre=loadA)
            chk("A")
            fence(rgR1)
            fence(rgR3)
            pbk = [0]

            def ws_proj(sv, sB, ntok0, nlen, evac):
                for n in range(2):
                    bk = (pbk[0] % 2) * 2 + n
                    t0 = ntok0 + n * nlen

                    def mm(e, bk=bk, t0=t0):
                        ins = None
                        for c in range(NCH):
                            ins = e.matmul(bank(bk)[:, 0:nlen], lhsT=sv[:, c, :], rhs=hT[:, c, t0:t0 + nlen],
                                           start=(c == 0), stop=(c == NCH - 1))
                        return ins
                    P.op(PE, mm, reads=sB + [hTB], writes=[bankB[bk]])
                    evac(n, bk)
                pbk[0] += 1

            for j in range(16):
                wb, sv = w_colchunk(w_in, j * 128)
                sv3 = sv.rearrange("p (c e) -> p c e", c=32)

                def evq(n, bk, j=j):
                    P.op(ACT, lambda e: e.activation(out=qT[:, j, n * NQ:(n + 1) * NQ], in_=bank(bk)[:, 0:NQ],
                                                     func=AF.Copy), reads=[bankB[bk]], writes=[qTB])
                ws_proj(sv3, wb, HALO, NQ, evq)
            chk('B_q')
            for h in range(4):
                src = w_in[:, 2048 + h * 64:2048 + (h + 1) * 64].rearrange("(c p) e -> p c e", p=128)
                wb, sv = wload(2, [(lambda sv: sv.rearrange("p (c e) -> p c e", c=32)[:, :, 0:64], src),
                                   (lambda sv: sv.rearrange("p (c e) -> p c e", c=32)[:, :, 64:128], src)])
                sv3 = sv.rearrange("p (c e) -> p c e", c=32)

                def evk(n, bk, h=h):
                    P.op(ACT, lambda e: e.activation(out=kT[:, h, n * 324:(n + 1) * 324], in_=bank(bk)[:, 0:324],
                                                     func=AF.Copy), reads=[bankB[bk]], writes=[kTB])
                ws_proj(sv3, wb, 0, 324, evk)
            chk('B_k')
            for jj in range(4):
                wb, sv = w_colchunk(w_in, 2048 + jj * 128)
                sv3 = sv.rearrange("p (c e) -> p c e", c=32)
                tiles = [4, 5] if jj < 2 else [0, 1, 2, 3, 4, 5]
                for tj in tiles:
                    rows = 128 if tj < 5 else NS
                    bk = 4 + (tj % 2)

                    def mm(e, tj=tj, rows=rows, bk=bk, sv3=sv3):
                        ins = None
                        for c in range(NCH):
                            ins = e.matmul(bank(bk)[0:rows, 0:128], lhsT=hT[:, c, tj * 128:tj * 128 + rows],
                                           rhs=sv3[:, c, :], start=(c == 0), stop=(c == NCH - 1))
                        return ins
                    P.op(PE, mm, reads=wb + [hTB], writes=[bankB[bk]])
                    if jj < 2:
                        dst = kst[0:rows, (tj - 4) * 256 + jj * 128:(tj - 4) * 256 + (jj + 1) * 128]
                        P.op(ACT, lambda e, dst=dst, bk=bk, rows=rows: e.activation(
                            out=dst, in_=bank(bk)[0:rows, 0:128], func=AF.Copy), reads=[bankB[bk]], writes=[B["kst"]])
                    else:
                        c0 = (jj - 2) * 128
                        P.op(ACT, lambda e, tj=tj, bk=bk, rows=rows, c0=c0: e.activation(
                            out=vtok[0:rows, tj, c0:c0 + 128], in_=bank(bk)[0:rows, 0:128], func=AF.Copy),
                            reads=[bankB[bk]], writes=[vtB])
                        if tj >= 4:
                            dst = vst[0:rows, (tj - 4) * 256 + c0:(tj - 4) * 256 + c0 + 128]
                            P.op(ACT, lambda e, dst=dst, bk=bk, rows=rows: e.activation(
                                out=dst, in_=bank(bk)[0:rows, 0:128], func=AF.Copy), reads=[bankB[bk]], writes=[B["vst"]])
            chk('B_kv')
            if p == NPASS - 1:
                P.op(SP, lambda e: e.dma_start(out=kwp_o, in_=kst[:, 0:256]), reads=[B["kst"]], dsem=nd('o'))
                P.op(SP, lambda e: e.dma_start(out=vwp_o, in_=vst[:, 0:256]), reads=[B["vst"]], dsem=nd('o'))
            krowB, vrowB = Buf("krow"), Buf("vrow")
            P.op(SP, lambda e, p=p: e.dma_start(out=kws_o[p * NS:(p + 1) * NS, 127, :], in_=kst[0:NS, 256:512]),
                 reads=[B["kst"]], writes=[krowB], dsem=nd('o'))
            P.op(SP, lambda e, p=p: e.dma_start(out=vws_o[p * NS:(p + 1) * NS, 127, :], in_=vst[0:NS, 256:512]),
                 reads=[B["vst"]], writes=[vrowB], dsem=nd('o'))
            chk('B_out')
            for j in range(16):
                wb, sv = w_colchunk(w_in, 2560 + j * 128)
                sv3 = sv.rearrange("p (c e) -> p c e", c=32)

                def evu(n, bk, j=j):
                    P.op(ACT, lambda e: e.activation(out=uT[:, j, n * 268:(n + 1) * 268], in_=bank(bk)[:, 0:268],
                                                     func=AF.Copy), reads=[bankB[bk]], writes=[uTB[j // 4]])
                ws_proj(sv3, wb, HALO - 16, 268, evu)
            chk('B_u')
            ustage = RXN[:, 0:2048]
            todo = [(528, NS, "s")]
            if p == NPASS - 1:
                todo.append((512, 16, "t"))
            for (t0, rows, kind) in todo:
                for j in range(16):
                    bk = 4 + j // 4
                    P.op(PE, lambda e, j=j, bk=bk, t0=t0, rows=rows: e.matmul(
                        bank(bk)[0:rows, (j % 4) * 128:(j % 4 + 1) * 128], lhsT=uT[:, j, t0:t0 + rows], rhs=idf,
                        start=True, stop=True), reads=[uTB[j // 4], B["idf"]], writes=[bankB[bk]])
                for q4 in range(4):
                    P.op(ACT, lambda e, q4=q4, rows=rows: e.activation(
                        out=ustage[0:rows, q4 * 512:(q4 + 1) * 512], in_=bank(4 + q4)[0:rows, :], func=AF.Copy),
                        reads=[bankB[4 + q4]], writes=[B["RXN"]])
                if kind == "s":
                    P.op(SP, lambda e, p=p: e.dma_start(out=pools_o[p * NS:(p + 1) * NS, 14, :], in_=ustage[0:NS, :]),
                         reads=[B["RXN"]], dsem=nd('o'))
                else:
                    P.op(SP, lambda e: e.dma_start(out=poolp_o, in_=ustage[0:16, :]), reads=[B["RXN"]], dsem=nd('o'))
            chk('B_ut')
            P.op(SP, lambda e, p=p: e.dma_start(out=pools_o[p * NS:(p + 1) * NS, 0:14, :],
                                                in_=spool[p * NS:(p + 1) * NS, 1:15, :]), dsem=nd('o'))

            chk("B")
            fence(rgR2)
            state = RXN[0:120, 0:2048]
            P.op(SP, lambda e, p=p: e.dma_start(out=state, in_=spool[p * NS:(p + 1) * NS, :, :].rearrange(
                "b j c -> (b j) c")), writes=[B["RXN"]], dsem=nd('m'))
            for j in range(16):
                g = j // 4
                P.op(PE, lambda e, j=j, g=g: e.matmul(bank(6)[:, j * 8:(j + 1) * 8],
                                                      lhsT=state[:, j * 128:(j + 1) * 128],
                                                      rhs=selt[0:120, g * 8:(g + 1) * 8], start=True, stop=True),
                     reads=[B["RXN"], B["selt"]], writes=[bankB[6]])
            P.op(DVE, lambda e: e.tensor_scalar(out=ssum, in0=bank(6)[:, 0:128], scalar1=1.0, scalar2=None, op0=ALU.mult), reads=[bankB[6]], writes=[B["ssum"]])
            ssum3 = ssum.rearrange("p (c b) -> p c b", c=16)
            for g in range(4):
                w = POOLW[g]
                U = uT[:, g * 4:(g + 1) * 4, :]
                UB = uTB[g]
                cur, curB = U, UB
                tmps = [(T1, T1B), (T2, T2B)]
                ti = 0
                s = 1
                while s < w:
                    lo = 2 * s - 1
                    dst, dstB = tmps[ti % 2]
                    P.op(DVE, lambda e, dst=dst, cur=cur, lo=lo, s=s: e.tensor_tensor(
                        out=dst[:, :, lo:528], in0=cur[:, :, lo:528], in1=cur[:, :, lo - s:528 - s], op=ALU.add),
                        reads=[curB], writes=[dstB])
                    cur, curB = dst, dstB
                    ti += 1
                    s *= 2
                WS, WSB = cur, curB
                P.op(DVE, lambda e, WS=WS, U=U, g=g: e.tensor_tensor(
                    out=WS[:, :, 528:536], in0=U[:, :, 528:536], in1=ssum3[:, g * 4:(g + 1) * 4, :], op=ALU.add),
                    reads=[UB, B["ssum"]], writes=[WSB])
                dd, ddB = dT[g % 2], dTB[g % 2]
                P.op(DVE, lambda e, WS=WS, U=U, dd=dd, w=w: e.scalar_tensor_tensor(
                    out=dd[:, :, 0:TQ], in0=WS[:, :, 16:536], scalar=1.0 / w, in1=U[:, :, 16:536],
                    op0=ALU.mult, op1=ALU.subtract), reads=[WSB, UB], writes=[ddB])
                ct3 = ctmp.rearrange("p (c t) -> p c t", c=4)
                ic = invc[:, (p * 4 + g) * 16:(p * 4 + g + 1) * 16]
                P.op(DVE, lambda e, WS=WS, ic=ic: e.tensor_tensor(
                    out=ct3, in0=WS[:, :, 16:32], in1=ic.unsqueeze(1).to_broadcast([128, 4, 16]), op=ALU.mult),
                    reads=[WSB, B["invc"]], writes=[B["ctmp"]])
                P.op(DVE, lambda e, U=U, dd=dd: e.tensor_tensor(
                    out=dd[:, :, 0:16], in0=ct3, in1=U[:, :, 16:32], op=ALU.subtract),
                    reads=[B["ctmp"], UB], writes=[ddB])
                wb, sv = wload(1, [(lambda sv: sv[:, 0:2048].rearrange("p (cc e) -> p cc e", cc=4),
                                    w_pool[g].rearrange("(cc p) e -> p cc e", p=128))])
                wp = sv[:, 0:2048].rearrange("p (cc e) -> p cc e", cc=4)
                for ec in range(4):
                    for n in range(2):
                        bk = (pbk[0] % 2) * 2 + n

                        def mm(e, ec=ec, n=n, bk=bk, wp=wp, dd=dd):
                            ins = None
                            for cc in range(4):
                                ins = e.matmul(bank(bk)[:, 0:NQ], lhsT=wp[:, cc, ec * 128:(ec + 1) * 128],
                                               rhs=dd[:, cc, n * NQ:(n + 1) * NQ], start=(cc == 0), stop=(cc == 3))
                            return ins
                        P.op(PE, mm, reads=wb + [ddB], writes=[bankB[bk]])
                        ch = g * 4 + ec
                        P.op(ACT, lambda e, ch=ch, n=n, bk=bk: e.activation(
                            out=catT[:, 16 + ch, n * NQ:(n + 1) * NQ], in_=bank(bk)[:, 0:NQ], func=AF.Copy,
                            scale=psT[:, ch:ch + 1]), reads=[bankB[bk], B["psT"]], writes=[catB])
                    pbk[0] += 1

            chk("C")
            fence(rgR1)
            P.op(SP, lambda e, p=p: e.dma_start(out=Kt[0:127, :, :], in_=ck[p * NS:(p + 1) * NS, 1:128, :].rearrange(
                "b s c -> s b c")), writes=[KtB], dsem=nd('m'))
            P.op(SP, lambda e, p=p: e.dma_start(out=Vt[0:127, :, :], in_=cv[p * NS:(p + 1) * NS, 1:128, :].rearrange(
                "b s c -> s b c")), writes=[VtB], dsem=nd('m'))
            P.op(SP, lambda e, p=p: e.dma_start(out=Kt[127:128, :, :], in_=kws_o[p * NS:(p + 1) * NS, 127:128, :].rearrange(
                "b o c -> o b c")), reads=[krowB], writes=[KtB], dsem=nd('m'))
            P.op(SP, lambda e, p=p: e.dma_start(out=Vt[127:128, :, :], in_=vws_o[p * NS:(p + 1) * NS, 127:128, :].rearrange(
                "b o c -> o b c")), reads=[vrowB], writes=[VtB], dsem=nd('m'))
            P.op(SP, lambda e, p=p: e.dma_start(out=kws_o[p * NS:(p + 1) * NS, 0:127, :].rearrange("b s c -> s b c"),
                                                in_=Kt[0:127, :, :]), reads=[KtB], dsem=nd('o'))
            P.op(SP, lambda e, p=p: e.dma_start(out=vws_o[p * NS:(p + 1) * NS, 0:127, :].rearrange("b s c -> s b c"),
                                                in_=Vt[0:127, :, :]), reads=[VtB], dsem=nd('o'))
            units = [(blk, kvh, hh) for blk in range(4) for kvh in range(4) for hh in range(2)]
            NU = len(units)
            EBp = [Buf("E0", rgR2), Buf("E1", rgR2)]
            ETBp = [Buf("ET0", rgR2), Buf("ET1", rgR2)]
            stt = {}

            def st_qk(t):
                blk, kvh, hh = units[t]
                par = t % 2
                maskb = mask0b if (p == 0 and blk == 0) else maskAb
                maskB_ = B["mask0b"] if (p == 0 and blk == 0) else B["maskAb"]
                q0 = blk * 128

                def qk(e):
                    ins = None
                    for rr in range(4):
                        h = kvh * 8 + hh * 4 + rr
                        chq, hf = h // 2, h % 2
                        out = bank(par * 2 + rr // 2)[:, (rr % 2) * 256:(rr % 2 + 1) * 256]
                        e.matmul(out, lhsT=qT[hf * 64:(hf + 1) * 64, chq, q0:q0 + 128],
                                 rhs=kT[hf * 64:(hf + 1) * 64, kvh, q0:q0 + 256], start=True, stop=False)
                        ins = e.matmul(out, lhsT=idb, rhs=maskb, start=False, stop=True)
                    return ins
                P.op(PE, qk, reads=[qTB, kTB, B["idb"], maskB_], writes=bankB[par * 2:par * 2 + 2])
                S4 = pa[:, par * 1024:(par + 1) * 1024].rearrange("p (r s) -> p r s", r=4)
                d = {k: stat(4) for k in ("mraw", "negm", "tt", "esk", "rs", "den", "rinv")}
                stt[t] = d
                h0 = kvh * 8 + hh * 4
                P.op(DVE, lambda e: e.tensor_reduce(out=d["mraw"][0], in_=S4, axis=AX.X, op=ALU.max),
                     reads=bankB[par * 2:par * 2 + 2], writes=[d["mraw"][1]])
                P.op(DVE, lambda e: e.scalar_tensor_tensor(
                    out=d["negm"][0], in0=d["mraw"][0], scalar=-0.125, in1=negsinkb[:, h0:h0 + 4], op0=ALU.mult,
                    op1=ALU.min), reads=[d["mraw"][1], B["negsinkb"]], writes=[d["negm"][1]])
                P.op(DVE, lambda e: e.tensor_tensor(out=d["tt"][0], in0=d["negm"][0], in1=sinkb[:, h0:h0 + 4],
                                                    op=ALU.add), reads=[d["negm"][1], B["sinkb"]], writes=[d["tt"][1]])
                P.op(ACT, lambda e: e.activation(out=d["esk"][0], in_=d["tt"][0], func=AF.Exp),
                     reads=[d["tt"][1]], writes=[d["esk"][1]])
                for rr in range(4):
                    P.op(ACT, lambda e, rr=rr: e.activation(
                        out=Eb[:, par * 4 + rr, :], in_=S4[:, rr, :], func=AF.Exp, bias=d["negm"][0][:, rr:rr + 1],
                        scale=0.125, accum_out=d["rs"][0][:, rr:rr + 1]),
                        reads=[bankB[par * 2 + rr // 2], d["negm"][1]], writes=[EBp[par], d["rs"][1]])

            def st_tr(t):
                par = t % 2
                d = stt[t]
                P.op(DVE, lambda e: e.tensor_tensor(out=d["den"][0], in0=d["rs"][0], in1=d["esk"][0], op=ALU.add),
                     reads=[d["rs"][1], d["esk"][1]], writes=[d["den"][1]])
                P.op(DVE, lambda e: e.reciprocal(out=d["rinv"][0], in_=d["den"][0]), reads=[d["den"][1]],
                     writes=[d["rinv"][1]])
                bk = 6 + par
                pv = bf(bank(bk))

                def trE(e):
                    ins = None
                    for rr in range(4):
                        for sc in range(2):
                            ins = e.transpose(out=pv[:, (rr * 2 + sc) * 128:(rr * 2 + sc + 1) * 128],
                                              in_=Eb[:, par * 4 + rr, sc * 128:(sc + 1) * 128], identity=idb)
                    return ins
                P.op(PE, trE, reads=[EBp[par], B["idb"]], writes=[bankB[bk]])
                dst = ETb[:, par * 1024:(par + 1) * 1024]
                if par == 0:
                    P.op(ACT, lambda e: e.activation(out=dst, in_=pv, func=AF.Copy), reads=[bankB[bk]],
                         writes=[ETBp[par]])
                else:
                    P.op(DVE, lambda e: e.tensor_scalar(out=dst, in0=pv, scalar1=1.0, scalar2=None, op0=ALU.mult),
                         reads=[bankB[bk]], writes=[ETBp[par]])

            def st_pv(t):
                blk, kvh, hh = units[t]
                par = t % 2
                d = stt.pop(t)
                ob = 4 + par

                def pvm(e):
                    ins = None
                    for rr in range(4):
                        for sc in range(2):
                            ins = e.matmul(bank(ob)[:, rr * 64:(rr + 1) * 64],
                                           lhsT=ETb[:, par * 1024 + (rr * 2 + sc) * 128:par * 1024 + (rr * 2 + sc + 1) * 128],
                                           rhs=vtok[:, blk + sc, kvh * 64:(kvh + 1) * 64],
                                           start=(sc == 0), stop=(sc == 1))
                    return ins
                P.op(PE, pvm, reads=[ETBp[par], vtB], writes=[bankB[ob]])
                c0 = (kvh * 8 + hh * 4) * 64
                P.op(DVE, lambda e: e.tensor_tensor(
                    out=a_tok[:, c0:c0 + 256].rearrange("p (r d) -> p r d", r=4),
                    in0=bank(ob)[:, 0:256].rearrange("p (r d) -> p r d", r=4),
                    in1=d["rinv"][0].unsqueeze(2).to_broadcast([128, 4, 64]), op=ALU.mult),
                    reads=[bankB[ob], d["rinv"][1]], writes=[B["a_tok"]])
                if kvh == 3 and hh == 1:
                    q0 = blk * 128
                    for half in range(2):
                        tb = 4 + half
                        pv = bf(bank(tb))

                        def trA(e, half=half, pv=pv):
                            ins = None
                            for k in range(8):
                                c = half * 8 + k
                                ins = e.transpose(out=pv[:, k * 128:(k + 1) * 128],
                                                  in_=a_tok[:, c * 128:(c + 1) * 128], identity=idb)
                            return ins
                        P.op(PE, trA, reads=[B["a_tok"], B["idb"]], writes=[bankB[tb]])
                        P.op(ACT, lambda e, half=half, pv=pv, q0=q0: e.activation(
                            out=catT[:, half * 8:(half + 1) * 8, q0:q0 + 128],
                            in_=pv.rearrange("p (c t) -> p c t", c=8), func=AF.Copy),
                            reads=[bankB[tb]], writes=[catB])

            for t in range(NU + 2):
                if t < NU:
                    st_qk(t)
                if 1 <= t <= NU:
                    st_tr(t - 1)
                if 2 <= t <= NU + 1:
                    st_pv(t - 2)

            chk("D")
            for half in range(2):
                pv = bf(bank(5))

                def trq(e, half=half, pv=pv):
                    ins = None
                    for k in range(8):
                        c = half * 8 + k
                        ins = e.transpose(out=pv[0:NS, k * 128:(k + 1) * 128], in_=qT[:, c, TP:TP + NS], identity=idb)
                    return ins
                P.op(PE, trq, reads=[qTB, B["idb"]], writes=[bankB[5]])
                P.op(ACT, lambda e, half=half, pv=pv: e.activation(
                    out=qtok[0:NS, half * 1024:(half + 1) * 1024], in_=pv[0:NS, :], func=AF.Copy),
                    reads=[bankB[5]], writes=[qtokB])
            for b in range(NS):
                def qbm(e, b=b):
                    ins = None
                    for k in range(4):
                        ins = e.matmul(bank(k), lhsT=onehot[0:NS, b * 128:(b + 1) * 128],
                                       rhs=qtok[0:NS, k * 512:(k + 1) * 512], start=True, stop=True)
                    return ins
                P.op(PE, qbm, reads=[qtokB, B["onehot"]], writes=bankB[0:4])
                P.op(DVE, lambda e, b=b: e.tensor_tensor(
                    out=prod.rearrange("p (k r d) -> p k r d", k=4, r=8),
                    in0=pa[:, :].rearrange("p (k r d) -> p k r d", k=4, r=8),
                    in1=Kt[:, b, :].rearrange("p (k d) -> p k d", k=4).unsqueeze(2).to_broadcast([128, 4, 8, 64]),
                    op=ALU.mult), reads=bankB[0:4] + [KtB], writes=[prodB])
                P.op(DVE, lambda e, b=b: e.tensor_reduce(
                    out=STt[:, b * 32:(b + 1) * 32], in_=prod.rearrange("p (h d) -> p h d", h=32), axis=AX.X,
                    op=ALU.add), reads=[prodB], writes=[STB])
            for half in range(2):
                P.op(PE, lambda e, half=half: e.matmul(bank(4)[:, half * 128:(half + 1) * 128],
                                                       lhsT=STt[:, half * 128:(half + 1) * 128], rhs=idf,
                                                       start=True, stop=True),
                     reads=[STB, B["idf"]], writes=[bankB[4]])
            S2 = bank(4)[:, 0:256].rearrange("p (a s) -> p a s", a=2)
            mraw, mrB = stat(2)
            negm, nmB = stat(2)
            tt, ttB = stat(2)
            esk, esB = stat(2)
            rs, rsB = stat(2)
            den, dnB = stat(2)
            rinv, riB = stat(2)
            P.op(DVE, lambda e: e.tensor_reduce(out=mraw, in_=S2, axis=AX.X, op=ALU.max), reads=[bankB[4]], writes=[mrB])
            P.op(DVE, lambda e: e.scalar_tensor_tensor(out=negm, in0=mraw, scalar=-0.125,
                                                       in1=negsinkrep.to_broadcast([128, 2]), op0=ALU.mult,
                                                       op1=ALU.min), reads=[mrB, B["negsinkrep"]], writes=[nmB])
            P.op(DVE, lambda e: e.tensor_tensor(out=tt, in0=negm, in1=sinkrep.to_broadcast([128, 2]), op=ALU.add),
                 reads=[nmB, B["sinkrep"]], writes=[ttB])
            P.op(ACT, lambda e: e.activation(out=esk, in_=tt, func=AF.Exp), reads=[ttB], writes=[esB])
            for a in range(2):
                P.op(ACT, lambda e, a=a: e.activation(out=Es[:, a * 128:(a + 1) * 128], in_=S2[:, a, :], func=AF.Exp,
                                                      bias=negm[:, a:a + 1], scale=0.125, accum_out=rs[:, a:a + 1]),
                     reads=[bankB[4], nmB], writes=[EsB, rsB])
            P.op(DVE, lambda e: e.tensor_tensor(out=den, in0=rs, in1=esk, op=ALU.add), reads=[rsB, esB], writes=[dnB])
            P.op(DVE, lambda e: e.reciprocal(out=rinv, in_=den), reads=[dnB], writes=[riB])
            P.op(DVE, lambda e: e.tensor_tensor(out=Ps.rearrange("p (a s) -> p a s", a=2),
                                                in0=Es.rearrange("p (a s) -> p a s", a=2),
                                                in1=rinv.unsqueeze(2).to_broadcast([128, 2, 128]), op=ALU.mult),
                 reads=[EsB, riB], writes=[PsB])
            for half in range(2):
                P.op(PE, lambda e, half=half: e.matmul(bank(5)[:, half * 128:(half + 1) * 128],
                                                       lhsT=Ps[:, half * 128:(half + 1) * 128], rhs=idf,
                                                       start=True, stop=True),
                     reads=[PsB, B["idf"]], writes=[bankB[5]])
            P.op(ACT, lambda e: e.activation(out=PTt, in_=bank(5)[:, 0:256], func=AF.Copy),
                 reads=[bankB[5]], writes=[PTB])
            for b in range(NS):
                P.op(DVE, lambda e, b=b: e.tensor_tensor(
                    out=prodv.rearrange("p (k r d) -> p k r d", k=4, r=8),
                    in0=Vt[:, b, :].rearrange("p (k d) -> p k d", k=4).unsqueeze(2).to_broadcast([128, 4, 8, 64]),
                    in1=PTt[:, b * 32:(b + 1) * 32].rearrange("p (k r) -> p k r", k=4).unsqueeze(3).to_broadcast(
                        [128, 4, 8, 64]), op=ALU.mult), reads=[VtB, PTB], writes=[prodvB])

                def pvs(e, b=b):
                    ins = None
                    for k in range(4):
                        ins = e.matmul(bank(k)[0:NS, :], lhsT=selcol[:, b * 8:(b + 1) * 8],
                                       rhs=prodv[:, k * 512:(k + 1) * 512], start=(b == 0), stop=(b == NS - 1))
                    return ins
                P.op(PE, pvs, reads=[prodvB, B["selcol"]], writes=bankB[0:4])
            P.op(ACT, lambda e: e.activation(out=a_tok[0:NS, :], in_=pa[0:NS, :], func=AF.Copy),
                 reads=bankB[0:4], writes=[B["a_tok"]])
            for half in range(2):
                pv = bf(bank(5))

                def trAs(e, half=half, pv=pv):
                    ins = None
                    for k in range(8):
                        c = half * 8 + k
                        ins = e.transpose(out=pv[:, k * 128:(k + 1) * 128], in_=a_tok[:, c * 128:(c + 1) * 128],
                                          identity=idb)
                    return ins
                P.op(PE, trAs, reads=[B["a_tok"], B["idb"]], writes=[bankB[5]])
                P.op(ACT, lambda e, half=half, pv=pv: e.activation(
                    out=catT[:, half * 8:(half + 1) * 8, TP:TP + NS],
                    in_=pv.rearrange("p (c t) -> p c t", c=8)[:, :, 0:NS], func=AF.Copy),
                    reads=[bankB[5]], writes=[catB])

            chk("D2")
            fence(rgR1)
            fence(rgR3)
            for j in range(4):
                P.op(SP, lambda e, j=j, p=p: e.dma_start(out=x1t[:, j, :],
                                                         in_=xh[HALO + p * TP + 128 * j:HALO + p * TP + 128 * (j + 1), :]),
                     writes=[x1B[j]], dsem=nd('x'))
            P.op(DVE, lambda e: e.memset(x1s, 0.0), writes=[x1sB])
            P.op(SP, lambda e, p=p: e.dma_start(out=x1s[0:NS, :], in_=xs[p * NS:(p + 1) * NS, :]),
                 writes=[x1sB], dsem=nd('x'))
            for g0 in range(0, NCH, G_ACC):
                proj_acc(lambda k, g0=g0: catT[:, g0 + k, :], catB, w_o, [ec * 128 for ec in range(g0, g0 + G_ACC)],
                         x1t, x1B, x1s, x1sB)
            chk("E")
            fence(rgR2)
            tilesF = []
            for j in range(5):
                rows = 128 if j < 4 else NS
                tilesF.append((x1t[:, j, :] if j < 4 else x1s, x1B[j] if j < 4 else x1sB, rows, 128 * j))
            norm_phase(tilesF, gffnT, B["gffnT"], h2T, h2B, (6, 7), junkA, junkAB)
            chk("F")
            gi = 0
            for g0 in range(0, NFC, G_ACC):
                fcs = list(range(g0, min(g0 + G_ACC, NFC)))
                fbuf, fB = ff[gi % 2], ffB[gi % 2]
                for k, fc in enumerate(fcs):
                    ig, svg = w_colchunk(w_gate, fc * 128)
                    iu, svu = w_colchunk(w_up, fc * 128)
                    svg3 = svg.rearrange("p (c e) -> p c e", c=32)
                    svu3 = svu.rearrange("p (c e) -> p c e", c=32)
                    for (sv3, si, b0) in ((svg3, ig, 0), (svu3, iu, 2)):
                        for n in range(2):
                            bk = b0 + n

                            def mm(e, sv3=sv3, n=n, bk=bk):
                                ins = None
                                for c in range(NCH):
                                    ins = e.matmul(bank(bk)[:, 0:NQ], lhsT=sv3[:, c, :],
                                                   rhs=h2T[:, c, n * NQ:(n + 1) * NQ], start=(c == 0),
                                                   stop=(c == NCH - 1))
                                return ins
                            P.op(PE, mm, reads=si + [h2B], writes=[bankB[bk]])
                    for n in range(2):
                        sg = sgt[:, n * NQ:(n + 1) * NQ]
                        sgB = B[f"sgt{n}"]
                        P.op(ACT, lambda e, n=n, sg=sg: e.activation(out=sg, in_=bank(n)[:, 0:NQ], func=AF.Silu),
                             reads=[bankB[n]], writes=[sgB])
                        P.op(DVE, lambda e, n=n, sg=sg, k=k, fbuf=fbuf: e.tensor_tensor(
                            out=fbuf[:, k, n * NQ:(n + 1) * NQ], in0=sg, in1=bank(2 + n)[:, 0:NQ], op=ALU.mult),
                            reads=[sgB, bankB[2 + n]], writes=[fB])
                proj_acc(lambda k, fbuf=fbuf: fbuf[:, k, :], fB, w_down, [fc * 128 for fc in fcs], x1t, x1B, x1s, x1sB)
                gi += 1

            chk("G")
            fence(rgR2)
            P.op(SP, lambda e: e.dma_start(out=gfin, in_=g_fin.partition_broadcast(128)), writes=[gfinB], dsem=nd('m'))
            for j in range(5):
                rows = 128 if j < 4 else NS
                src = x1t[:, j, :] if j < 4 else x1s
                srcB = x1B[j] if j < 4 else x1sB
                rstd, rsB = norm_tile(src, srcB, rows, None, None, None, None, 0, None)
                P.op(DVE, lambda e, src=src, rstd=rstd: e.scalar_tensor_tensor(
                    out=src, in0=src, scalar=rstd, in1=gfin, op0=ALU.mult, op1=ALU.mult),
                    reads=[rsB, gfinB], writes=[srcB])
                src = src[0:rows, :]
                if j < 4:
                    dst = y_o[p * TP + 128 * j:p * TP + 128 * (j + 1), :]
                else:
                    dst = ys_o[p * NS:(p + 1) * NS, :]
                P.op(SP, lambda e, dst=dst, src=src: e.dma_start(out=dst, in_=src), reads=[srcB], dsem=nd('o'))
            chk("H")
          except _Stop:
            break

        P.emit(nc, es)
    return nc


_NC_CACHE = {}


def _consts():
    qi = np.arange(128)[:, None]
    kj = np.arange(256)[None, :] - 128
    band = (kj <= qi) & (qi - kj < 128)
    maskA = np.where(band, 0.0, MASKNEG).astype(np.float32)
    mask_start = np.where(band & (kj >= 0), 0.0, MASKNEG).astype(np.float32)
    sel = np.zeros((120, 4, 8), np.float32)
    for b in range(8):
        for j in range(15):
            for g, w in enumerate(POOLW):
                if j >= 16 - w:
                    sel[b * 15 + j, g, b] = 1.0
    onehot = np.zeros((8, 8, 128), np.float32)
    for b in range(8):
        onehot[b, b, :] = 1.0
    selcol = np.zeros((128, 8, 8), np.float32)
    for b in range(8):
        selcol[:, b, b] = 1.0
    return maskA, mask_start, sel.reshape(120, 32), onehot.reshape(8, 1024), selcol.reshape(128, 64)


def kernel(x_prompt, x_sample, cache_k_win, cache_v_win, state_pool, g_mix, w_in, sinks, w_pool, pool_scale,
           w_o, g_ffn, w_gate, w_up, w_down, g_final):
    f = lambda a: np.ascontiguousarray(np.asarray(a, dtype=np.float32))
    x_prompt, x_sample = f(x_prompt), f(x_sample)
    cache_k_win, cache_v_win, state_pool = f(cache_k_win), f(cache_v_win), f(state_pool)
    if "nc" not in _NC_CACHE:
        _NC_CACHE["nc"] = build_program()
    nc = _NC_CACHE["nc"]
    maskA, mask_start, sel, onehot, selcol = _consts()
    shared = {
        "g_mix": f(g_mix)[0], "g_ffn": f(g_ffn)[0], "g_fin": f(g_final), "w_in": f(w_in)[0], "w_pool": f(w_pool)[0],
        "pscale": f(pool_scale)[0], "w_o": f(w_o)[0], "w_gate": f(w_gate)[0], "w_up": f(w_up)[0],
        "w_down": f(w_down)[0], "sinks": f(sinks)[0], "sinkrep": np.tile(f(sinks)[0], 4),
        "ident": np.eye(128, dtype=np.float32), "maskA": maskA, "sel": sel, "onehot": onehot, "selcol": selcol,
    }
    in_maps = []
    NSC = NPASS * NS
    for c in range(NCORES):
        b, half = c // 2, c % 2
        xh = np.zeros((HALO + NPASS * TP, D), np.float32)
        if half == 0:
            xh[HALO:] = x_prompt[b, 0:1024]
        else:
            xh[:] = x_prompt[b, 1024 - HALO:2048]
        invc = np.zeros((NPASS, 4, 16), np.float32)
        for p in range(NPASS):
            pos = half * 1024 + p * TP + np.arange(16)
            for g, w in enumerate(POOLW):
                invc[p, g] = 1.0 / np.minimum(pos + 1, w)
        m = dict(shared)
        m.update({
            "xh": xh, "xs": np.ascontiguousarray(x_sample[c * NSC:(c + 1) * NSC, 0, :]),
            "ck": np.ascontiguousarray(cache_k_win[0, c * NSC:(c + 1) * NSC].reshape(NSC, 128, 256)),
            "cv": np.ascontiguousarray(cache_v_win[0, c * NSC:(c + 1) * NSC].reshape(NSC, 128, 256)),
            "sp": np.ascontiguousarray(state_pool[0, c * NSC:(c + 1) * NSC]),
            "mask0": mask_start if half == 0 else maskA, "invc": invc.reshape(-1),
        })
        in_maps.append(m)
    if _NC_CACHE.get("trace_hook") is not None:
        return _NC_CACHE["trace_hook"](in_maps)
    res = run_bass_kernel_spmd(nc, in_maps, core_ids=list(range(NCORES)))
    R = res.results
    y_prompt = np.zeros((4, 2048, D), np.float32)
    y_sample = np.zeros((128, 1, D), np.float32)
    kwp = np.zeros((1, 4, 128, 4, 64), np.float32)
    vwp = np.zeros((1, 4, 128, 4, 64), np.float32)
    poolp = np.zeros((1, 4, 15, 2048), np.float32)
    kws = np.zeros((1, 128, 128, 4, 64), np.float32)
    vws = np.zeros((1, 128, 128, 4, 64), np.float32)
    pools = np.zeros((1, 128, 15, 2048), np.float32)
    for c in range(NCORES):
        b, half = c // 2, c % 2
        r = R[c]
        y_prompt[b, half * 1024:(half + 1) * 1024] = r["y"]
        y_sample[c * NSC:(c + 1) * NSC, 0] = r["ys"]
        if half == 1:
            kwp[0, b] = r["kwp"].reshape(128, 4, 64)
            vwp[0, b] = r["vwp"].reshape(128, 4, 64)
            poolp[0, b] = r["poolp"][1:16]
        kws[0, c * NSC:(c + 1) * NSC] = r["kws"].reshape(NSC, 128, 4, 64)
        vws[0, c * NSC:(c + 1) * NSC] = r["vws"].reshape(NSC, 128, 4, 64)
        pools[0, c * NSC:(c + 1) * NSC] = r["pools"]
    return (y_prompt, y_sample, kwp, vwp, poolp, kws, vws, pools)
```
